# Optimizing a Trainium2 kernel written in Bass

```python
import math
import jax, jax.numpy as jnp
from jax import lax
import numpy as np

D_MODEL = 1024
BATCH = 2
SEQ = 8192
DEPTH = 2

HEAD_DIM = 64
MIX_WIDTH = D_MODEL
ATTN_HEADS = 6
ATTN_KV_HEADS = 2
ATTN_GROUP = ATTN_HEADS // ATTN_KV_HEADS
ATTN_WIDTH = ATTN_HEADS * HEAD_DIM
ATTN_KV_WIDTH = ATTN_KV_HEADS * HEAD_DIM
WINDOW = 128
BLOCK_Q = 128
N_BUCKETS = 32
MAX_EXACT = N_BUCKETS // 2
MAX_DISTANCE = WINDOW
RWKV_HEADS = 6
RWKV_HEAD_DIM = 64
RWKV_WIDTH = RWKV_HEADS * RWKV_HEAD_DIM
D_DECAY_LORA = 32
D_AAA_LORA = 32
D_MV_LORA = 16
D_GATE_LORA = 64
RWKV_SHIFT_COLS = 3 * RWKV_WIDTH + D_DECAY_LORA + D_AAA_LORA + D_GATE_LORA
MEM_TOKENS = 256
MEM_HEADS = 4
MEM_WIDTH = MIX_WIDTH - ATTN_WIDTH - RWKV_WIDTH
IN_BASE = ATTN_WIDTH + 2 * ATTN_KV_WIDTH + RWKV_SHIFT_COLS + MEM_WIDTH
D_FF = 2816
CONV_WIDTH = 3
EPS = 1e-6
GN_EPS = 64e-5
L2_EPS = 1e-12

kernel_name = "hybrid_swa_rwkv7_memory_convffn"


def split_cols(t, sizes):
    return jnp.split(t, [int(c) for c in np.cumsum(sizes)[:-1]], axis=-1)


def rms_norm(x, g):
    xf = x.astype(jnp.float32)
    y = xf * lax.rsqrt(jnp.mean(xf * xf, axis=-1, keepdims=True) + EPS)
    return (y * g.astype(jnp.float32)).astype(x.dtype)


def token_shift(p, mu):
    prev = jnp.pad(p, ((0, 0), (1, 0), (0, 0)))[:, :-1]
    return p + mu * (prev - p)


def t5_causal_bucket(dist):
    is_small = dist < MAX_EXACT
    d = jnp.maximum(dist, 1).astype(jnp.float32)
    large = MAX_EXACT + (jnp.log(d / MAX_EXACT) / math.log(MAX_DISTANCE / MAX_EXACT)
                         * (N_BUCKETS - MAX_EXACT)).astype(jnp.int32)
    large = jnp.minimum(large, N_BUCKETS - 1)
    return jnp.where(is_small, dist, large)


def sliding_window_attention(q, k, v, sinks, rel_bias):
    b, s = q.shape[0], q.shape[1]
    nb = s // BLOCK_Q
    qb = q.reshape(b, nb, BLOCK_Q, ATTN_KV_HEADS, ATTN_GROUP, HEAD_DIM)

    def band(t):
        tb = t.reshape(b, nb, BLOCK_Q, ATTN_KV_HEADS, HEAD_DIM)
        prev = jnp.concatenate([jnp.zeros_like(tb[:, :1]), tb[:, :-1]], axis=1)
        return jnp.concatenate([prev, tb], axis=2)

    kw, vw = band(k), band(v)
    logits = jnp.einsum('bnqhgd,bnkhd->bnhgqk', qb, kw,
                        preferred_element_type=jnp.float32) * (HEAD_DIM ** -0.5)
    qi = jnp.arange(BLOCK_Q)[:, None]
    kj = jnp.arange(2 * BLOCK_Q)[None, :]
    dist = qi + BLOCK_Q - kj
    in_band = (dist >= 0) & (dist < WINDOW)
    key_pos = jnp.arange(nb)[:, None, None] * BLOCK_Q - BLOCK_Q + kj[None]
    valid = in_band[None] & (key_pos >= 0)
    bucket = t5_causal_bucket(jnp.maximum(dist, 0))
    bias = rel_bias.astype(jnp.float32)[bucket]
    bias = bias.transpose(2, 0, 1).reshape(ATTN_KV_HEADS, ATTN_GROUP, BLOCK_Q, 2 * BLOCK_Q)
    logits = jnp.where(valid[None, :, None, None], logits + bias, -jnp.inf)
    sink = sinks.astype(jnp.float32).reshape(ATTN_KV_HEADS, ATTN_GROUP)[None, None, :, :, None, None]
    m = jnp.maximum(jnp.max(logits, axis=-1, keepdims=True), sink)
    e = jnp.exp(logits - m)
    p = e / (jnp.sum(e, axis=-1, keepdims=True) + jnp.exp(sink - m))
    out = jnp.einsum('bnhgqk,bnkhd->bnqhgd', p.astype(v.dtype), vw)
    return out.reshape(b, s, ATTN_WIDTH)


def rwkv7_scan(r, w, k, v, a, bvec):
    bsz = r.shape[0]
    decay = jnp.exp(-jnp.exp(w))

    def step(state, inp):
        r_t, d_t, k_t, v_t, a_t, b_t = inp
        sa = jnp.einsum('bhvk,bhk->bhv', state, a_t)
        state = (state * d_t[:, :, None, :] + sa[..., None] * b_t[:, :, None, :]
                 + v_t[..., None] * k_t[:, :, None, :])
        return state, jnp.einsum('bhvk,bhk->bhv', state, r_t)

    xs = tuple(jnp.moveaxis(t, 1, 0) for t in (r, decay, k, v, a, bvec))
    s0 = jnp.zeros((bsz, RWKV_HEADS, RWKV_HEAD_DIM, RWKV_HEAD_DIM), jnp.float32)
    _, y = lax.scan(step, s0, xs)
    return jnp.moveaxis(y, 0, 1)


def rwkv7_mixer(pb, value_residual, w0, w2, a0, a2, g2, k_k, k_a, r_k, ln_w, ln_b):
    out_dtype = pb.dtype
    f32 = lambda t: t.astype(jnp.float32)
    r, k, v, wd, ad, gd = split_cols(f32(pb), [RWKV_WIDTH, RWKV_WIDTH, RWKV_WIDTH,
                                               D_DECAY_LORA, D_AAA_LORA, D_GATE_LORA])
    w = -jax.nn.softplus(-(f32(w0) + jnp.tanh(wd) @ f32(w2))) - 0.5
    a = jax.nn.sigmoid(f32(a0) + ad @ f32(a2))
    g = jax.nn.sigmoid(gd) @ f32(g2)
    if value_residual is not None:
        v_first, v_down, v0, v2 = value_residual
        v = v + (f32(v_first) - v) * jax.nn.sigmoid(f32(v0) + f32(v_down) @ f32(v2))
    v_out = v
    b, s = pb.shape[0], pb.shape[1]
    heads = lambda t: t.reshape(b, s, RWKV_HEADS, RWKV_HEAD_DIM)
    kk = heads(k * f32(k_k))
    kk = kk / jnp.maximum(jnp.sqrt(jnp.sum(kk * kk, axis=-1, keepdims=True)), L2_EPS)
    k = k * (1.0 + (a - 1.0) * f32(k_a))
    rh, kh, vh, ah = heads(r), heads(k), heads(v), heads(a)
    y = rwkv7_scan(rh, heads(w), kh, vh, -kk, kk * ah)
    mu = jnp.mean(y, axis=-1, keepdims=True)
    var = jnp.mean(jnp.square(y - mu), axis=-1, keepdims=True)
    y = ((y - mu) * lax.rsqrt(var + GN_EPS)).reshape(b, s, RWKV_WIDTH) * f32(ln_w) + f32(ln_b)
    bonus = jnp.sum(rh * kh * f32(r_k), axis=-1, keepdims=True) * vh
    y = (y + bonus.reshape(b, s, RWKV_WIDTH)) * g
    return y.astype(out_dtype), v_out


def memory_attention(q, mk, mv):
    logits = jnp.einsum('bshd,bmhd->bhsm', q, mk,
                        preferred_element_type=jnp.float32) * (HEAD_DIM ** -0.5)
    p = jax.nn.softmax(logits, axis=-1)
    out = jnp.einsum('bhsm,bmhd->bshd', p.astype(mv.dtype), mv)
    return out.reshape(q.shape[0], q.shape[1], MEM_WIDTH)


def conv_ffn(h, w_up, conv_w, conv_b, w_down):
    u = h @ w_up
    s = u.shape[1]
    up_pad = jnp.pad(u, ((0, 0), (CONV_WIDTH - 1, 0), (0, 0)))
    uc = conv_b + sum(conv_w[i] * up_pad[:, i:i + s] for i in range(CONV_WIDTH))
    gate, val = jnp.split(uc, 2, axis=-1)
    return (jax.nn.silu(gate) * val) @ w_down


def setup_inputs(seed: int = 0) -> dict:
    key = jax.random.key(seed)
    ks = iter(jax.random.split(key, 40))
    n = lambda shape: jax.random.normal(next(ks), shape, jnp.float32)
    L, Lv = DEPTH, DEPTH - 1
    C = RWKV_WIDTH
    return {
        "x": n((BATCH, SEQ, D_MODEL)),
        "mem": n((BATCH, MEM_TOKENS, D_MODEL)),
        "rel_bias": 0.5 * n((N_BUCKETS, ATTN_HEADS)),
        "mix_norm_g": 1.0 + 0.1 * n((L, D_MODEL)),
        "w_in": n((L, D_MODEL, IN_BASE)) * D_MODEL ** -0.5,
        "w_in_vres": n((Lv, D_MODEL, D_MV_LORA)) * D_MODEL ** -0.5,
        "attn_q_norm": 1.0 + 0.1 * n((L, HEAD_DIM)),
        "attn_k_norm": 1.0 + 0.1 * n((L, HEAD_DIM)),
        "attn_sinks": n((L, ATTN_HEADS)),
        "rwkv_mu": jax.random.uniform(next(ks), (L, RWKV_SHIFT_COLS), jnp.float32),
        "rwkv_mu_vres": jax.random.uniform(next(ks), (Lv, D_MV_LORA), jnp.float32),
        "rwkv_w0": -2.0 + 0.5 * n((L, C)),
        "rwkv_w2": 0.5 * n((L, D_DECAY_LORA, C)) * D_DECAY_LORA ** -0.5,
        "rwkv_a0": 0.5 * n((L, C)),
        "rwkv_a2": 0.5 * n((L, D_AAA_LORA, C)) * D_AAA_LORA ** -0.5,
        "rwkv_v0": 0.5 * n((Lv, C)),
        "rwkv_v2": 0.5 * n((Lv, D_MV_LORA, C)) * D_MV_LORA ** -0.5,
        "rwkv_g2": n((L, D_GATE_LORA, C)) * D_GATE_LORA ** -0.5,
        "rwkv_k_k": 0.85 + 0.1 * n((L, C)),
        "rwkv_k_a": 1.0 + 0.1 * n((L, C)),
        "rwkv_r_k": 0.1 * n((L, RWKV_HEADS, RWKV_HEAD_DIM)),
        "rwkv_ln_w": 1.0 + 0.1 * n((L, C)),
        "rwkv_ln_b": 0.02 * n((L, C)),
        "mem_norm_g": 1.0 + 0.1 * n((L, D_MODEL)),
        "w_mem_kv": n((L, D_MODEL, 2 * MEM_WIDTH)) * D_MODEL ** -0.5,
        "mem_q_norm": 1.0 + 0.1 * n((L, HEAD_DIM)),
        "mem_k_norm": 1.0 + 0.1 * n((L, HEAD_DIM)),
        "w_out": n((L, MIX_WIDTH, D_MODEL)) * MIX_WIDTH ** -0.5,
        "ffn_norm_g": 1.0 + 0.1 * n((L, D_MODEL)),
        "w_up": n((L, D_MODEL, 2 * D_FF)) * D_MODEL ** -0.5,
        "conv_w": n((L, CONV_WIDTH, 2 * D_FF)) * CONV_WIDTH ** -0.5,
        "conv_b": 0.02 * n((L, 2 * D_FF)),
        "w_down": n((L, D_FF, D_MODEL)) * D_FF ** -0.5,
    }


def reference(x, mem, rel_bias, mix_norm_g, w_in, w_in_vres, attn_q_norm, attn_k_norm, attn_sinks,
              rwkv_mu, rwkv_mu_vres, rwkv_w0, rwkv_w2, rwkv_a0, rwkv_a2, rwkv_v0, rwkv_v2, rwkv_g2,
              rwkv_k_k, rwkv_k_a, rwkv_r_k, rwkv_ln_w, rwkv_ln_b, mem_norm_g, w_mem_kv,
              mem_q_norm, mem_k_norm, w_out, ffn_norm_g, w_up, conv_w, conv_b, w_down):
    b, s = x.shape[0], x.shape[1]
    m_tok = mem.shape[1]
    v_first = None
    for l in range(DEPTH):
        h = rms_norm(x, mix_norm_g[l])
        w_l = w_in[l] if l == 0 else jnp.concatenate([w_in[l], w_in_vres[l - 1]], axis=1)
        proj = h @ w_l
        sizes = [ATTN_WIDTH, ATTN_KV_WIDTH, ATTN_KV_WIDTH, RWKV_SHIFT_COLS, MEM_WIDTH]
        if l > 0:
            sizes = sizes + [D_MV_LORA]
        parts = split_cols(proj, sizes)
        qa, ka, va, pb, qm = parts[:5]

        qa = rms_norm(qa.reshape(b, s, ATTN_HEADS, HEAD_DIM), attn_q_norm[l])
        ka = rms_norm(ka.reshape(b, s, ATTN_KV_HEADS, HEAD_DIM), attn_k_norm[l])
        va = va.reshape(b, s, ATTN_KV_HEADS, HEAD_DIM)
        out_a = sliding_window_attention(qa, ka, va, attn_sinks[l], rel_bias)

        pb = token_shift(pb, rwkv_mu[l])
        if l == 0:
            vres = None
        else:
            v_down = token_shift(parts[5], rwkv_mu_vres[l - 1])
            vres = (v_first, v_down, rwkv_v0[l - 1], rwkv_v2[l - 1])
        out_b, v_l = rwkv7_mixer(pb, vres, rwkv_w0[l], rwkv_w2[l], rwkv_a0[l], rwkv_a2[l],
                                 rwkv_g2[l], rwkv_k_k[l], rwkv_k_a[l], rwkv_r_k[l],
                                 rwkv_ln_w[l], rwkv_ln_b[l])
        if l == 0:
            v_first = v_l

        mkv = rms_norm(mem, mem_norm_g[l]) @ w_mem_kv[l]
        mk, mv = jnp.split(mkv, 2, axis=-1)
        mk = rms_norm(mk.reshape(b, m_tok, MEM_HEADS, HEAD_DIM), mem_k_norm[l])
        mv = mv.reshape(b, m_tok, MEM_HEADS, HEAD_DIM)
        qm = rms_norm(qm.reshape(b, s, MEM_HEADS, HEAD_DIM), mem_q_norm[l])
        out_m = memory_attention(qm, mk, mv)

        x = x + jnp.concatenate([out_a, out_b, out_m], axis=-1) @ w_out[l]

        x = x + conv_ffn(rms_norm(x, ffn_norm_g[l]), w_up[l], conv_w[l], conv_b[l], w_down[l])
    return x
```

```python
import math
import contextlib
import numpy as np
import concourse.bass as bass
import concourse.mybir as mybir
from concourse.bass_utils import run_bass_kernel_spmd

F32 = mybir.dt.float32
BF16 = mybir.dt.bfloat16
AF = mybir.ActivationFunctionType
ALU = mybir.AluOpType

NCORE = 8
NT = 2048
NB = 16
NE = NT + 128
D = 1024
EPS = 1e-6
NEG = -30000.0
C0 = math.exp(-0.5)


class Buf:
    __slots__ = ("name", "last_w", "readers")

    def __init__(self, name):
        self.name = name
        self.last_w = None
        self.readers = []


N_DMA_SEMS = 24


class _Recorder:
    def __init__(self):
        self.call = None

    def __getattr__(self, name):
        def f(*a, **k):
            self.call = (name, a, k)
            return self
        return f


class Sched:
    ENG = ("pe", "act", "dve", "pool", "sp")

    def __init__(self, nc):
        self.nc = nc
        self.ops = []
        self.dma_ops = []

    def handle(self, eng):
        nc = self.nc
        return {"pe": nc.tensor, "act": nc.scalar, "dve": nc.vector,
                "pool": nc.gpsimd, "sp": nc.sync}[eng]

    def op(self, eng, fn, reads=(), writes=(), dma=False):
        rec = _Recorder()
        fn(rec)
        name, a, k = rec.call
        fn = (lambda h, name=name, a=a, k=k: getattr(h, name)(*a, **k))
        idx = len(self.ops)
        deps = set()
        for b in reads:
            if b.last_w is not None:
                deps.add(b.last_w)
        for b in writes:
            if b.last_w is not None:
                deps.add(b.last_w)
            for r in b.readers:
                deps.add(r)
        slot = None
        if dma:
            n = len(self.dma_ops)
            slot = n % N_DMA_SEMS
            if n - N_DMA_SEMS >= 0:
                deps.add(self.dma_ops[n - N_DMA_SEMS])
            self.dma_ops.append(idx)
        self.ops.append(dict(eng=eng, fn=fn, deps=deps, dma=dma, slot=slot))
        for b in reads:
            b.readers.append(idx)
        for b in writes:
            b.last_w = idx
            b.readers = []
        return idx

    def emit(self, final_wait_eng="sp"):
        nc = self.nc
        ops = self.ops
        for o in ops:
            nd = set()
            for d in o["deps"]:
                p = ops[d]
                if (not p["dma"]) and (not o["dma"]) and p["eng"] == o["eng"] == "pe":
                    continue
                nd.add(d)
            o["deps"] = nd
        for o in ops:
            o["sig"] = o["dma"]
        for o in ops:
            for d in o["deps"]:
                ops[d]["sig"] = True
        last = {}
        for i, o in enumerate(ops):
            key = ("dma", o["slot"]) if o["dma"] else o["eng"]
            last[key] = i
        for k, i in last.items():
            ops[i]["sig"] = True
        self._stack = contextlib.ExitStack()
        sems = {e: self._stack.enter_context(nc.semaphore("s_" + e)) for e in ("pe", "act", "dve", "pool", "sp")}
        dsems = [self._stack.enter_context(nc.semaphore("s_dma%d" % k)) for k in range(N_DMA_SEMS)]
        cnt = {e: 0 for e in sems}
        dcnt = [0] * N_DMA_SEMS
        for o in ops:
            if not o["sig"]:
                continue
            if o["dma"]:
                dcnt[o["slot"]] += 16
                o["sem"] = dsems[o["slot"]]
                o["val"] = dcnt[o["slot"]]
                o["semkey"] = ("dma", o["slot"])
            else:
                cnt[o["eng"]] += 1
                o["sem"] = sems[o["eng"]]
                o["val"] = cnt[o["eng"]]
                o["semkey"] = o["eng"]
        waited = {e: {} for e in self.ENG}
        nwaits = 0
        for o in ops:
            e = o["eng"]
            h = self.handle(e)
            need = {}
            for d in o["deps"]:
                p = ops[d]
                k = p["semkey"]
                if need.get(k, (0, None))[0] < p["val"]:
                    need[k] = (p["val"], p["sem"])
            for k, (v, s) in need.items():
                if waited[e].get(k, 0) >= v:
                    continue
                h.wait_ge(s, v)
                nwaits += 1
                waited[e][k] = v
            ins = o["fn"](h)
            if o["sig"]:
                ins.then_inc(o["sem"], 16 if o["dma"] else 1)
        h = self.handle(final_wait_eng)
        for k, i in last.items():
            p = ops[i]
            if waited[final_wait_eng].get(k, 0) >= p["val"]:
                continue
            h.wait_ge(p["sem"], p["val"])
        self.stats = dict(n_ops=len(ops), n_waits=nwaits, cnt=cnt, ndma=len(self.dma_ops))
        return self.stats


class T:
    def __init__(self, t, name):
        self.t = t
        self.b = Buf(name)

    def __getitem__(self, k):
        return self.t[k]


class KB:
    def __init__(self):
        self.nc = bass.Bass("TRN2", target_bir_lowering=False)
        self.S = Sched(self.nc)
        self.n = 0
        self.psf_pool = []
        self.psb_pool = []
        self.psf_i = 0
        self.psb_i = 0
        self.dq = 0

    def din(self, name, shape, dt=F32):
        return self.nc.dram_tensor(name, list(shape), dt, kind="ExternalInput").ap()

    def dout(self, name, shape, dt=F32):
        return self.nc.dram_tensor(name, list(shape), dt, kind="ExternalOutput").ap()

    def sb(self, name, shape, dt=F32):
        return T(self.nc.alloc_sbuf_tensor(name, list(shape), dt), name)

    def init_psum(self, nf=6, nb=2):
        for i in range(nf):
            self.psf_pool.append(T(self.nc.alloc_psum_tensor("psf%d" % i, [128, 512], F32), "psf%d" % i))
        for i in range(nb):
            self.psb_pool.append(T(self.nc.alloc_psum_tensor("psb%d" % i, [128, 1024], BF16), "psb%d" % i))

    def psf(self):
        p = self.psf_pool[self.psf_i % len(self.psf_pool)]
        self.psf_i += 1
        return p

    def psb(self):
        p = self.psb_pool[self.psb_i % len(self.psb_pool)]
        self.psb_i += 1
        return p

    def op(self, eng, fn, reads=(), writes=()):
        return self.S.op(eng, fn, [x.b for x in reads], [x.b for x in writes])

    def dma(self, out, in_, reads=(), writes=(), q=None):
        if q is None:
            q = "sp"
        return self.S.op(q, lambda h: h.dma_start(out=out, in_=in_), [x.b for x in reads], [x.b for x in writes], dma=True)

    def load(self, name, src_ap, shape, dt=F32):
        t = self.sb("sb_" + name, shape, dt)
        self.dma(t[:], src_ap, writes=[t])
        return t


def emit_consts(kb, need_bf_ident=True):
    c = {}
    c["identf"] = kb.load("identf", kb.din("identf", [128, 128])[:, :], [128, 128])
    c["blk64"] = kb.load("blk64", kb.din("blk64", [128, 128])[:, :], [128, 128])
    c["blk1"] = kb.load("blk1", kb.din("blk1", [128, 128])[:, :], [128, 128])
    if need_bf_ident:
        ib = kb.sb("identb", [128, 128], BF16)
        kb.op("dve", lambda h: h.tensor_copy(ib[:], c["identf"][:]), reads=[c["identf"]], writes=[ib])
        c["identb"] = ib
    return c


def const_arrays():
    blk64 = np.zeros((128, 128), np.float32)
    blk64[:64, :64] = 1.0 / 64
    blk64[64:, 64:] = 1.0 / 64
    blk1 = (blk64 > 0).astype(np.float32)
    return dict(identf=np.eye(128, dtype=np.float32), blk64=blk64, blk1=blk1)


def norm_transpose(kb, c, src, nblk, hT, tag, scratch=None):
    if scratch is None:
        xin = [kb.sb("xin%s%d" % (tag, i), [128, D]) for i in range(2)]
        junk = kb.sb("junk" + tag, [128, D])
        xb = [kb.sb("xb%s%d" % (tag, i), [128, D], BF16) for i in range(2)]
    else:
        xin, junk = scratch
        xb = [kb.sb("xb%s%d" % (tag, i), [128, D], BF16) for i in range(2)]
    st = [[kb.sb("st%s%d_%d" % (tag, i, k), [128, 1]) for k in range(3)] for i in range(2)]
    for blk in range(nblk):
        xt = xin[blk % 2]
        s0, s1, s2 = st[blk % 2]
        xbb = xb[blk % 2]
        kb.dma(xt[:, 0:D], src[blk * 128:(blk + 1) * 128, :], writes=[xt])
        kb.op("pool", lambda h, s0=s0: h.memset(s0[:], 0.0), writes=[s0])
        kb.op("act", lambda h, xt=xt, s0=s0: h.activation(out=junk[:, 0:D], in_=xt[:, 0:D], func=AF.Square, accum_out=s0[:]),
              reads=[xt, s0], writes=[junk, s0])
        kb.op("dve", lambda h, s0=s0, s1=s1: h.tensor_scalar(out=s1[:], in0=s0[:], scalar1=1.0 / D, scalar2=EPS,
                                                            op0=ALU.mult, op1=ALU.add), reads=[s0], writes=[s1])
        kb.op("act", lambda h, s1=s1, s2=s2: h.activation(out=s2[:], in_=s1[:], func=AF.Sqrt), reads=[s1], writes=[s2])
        kb.op("dve", lambda h, s1=s1, s2=s2: h.reciprocal(out=s1[:], in_=s2[:]), reads=[s2], writes=[s1])
        kb.op("dve", lambda h, xt=xt, s1=s1, xbb=xbb: h.tensor_scalar(out=xbb[:, 0:D], in0=xt[:, 0:D], scalar1=s1[:, 0:1], scalar2=None,
                                                                      op0=ALU.mult), reads=[xt, s1], writes=[xbb])
        pst = kb.psb()
        for kc in range(8):
            kb.op("pe", lambda h, kc=kc, pst=pst, xbb=xbb: h.transpose(out=pst[:, kc * 128:(kc + 1) * 128],
                                                                       in_=xbb[:, kc * 128:(kc + 1) * 128],
                                                                       identity=c["identb"][:]),
                  reads=[xbb, c["identb"]], writes=[pst])
        kb.op("act", lambda h, pst=pst, blk=blk: h.copy(out=hT[:, :, blk * 128:(blk + 1) * 128],
                                                        in_=pst[:].rearrange("p (k t) -> p k t", k=8)),
              reads=[pst], writes=[hT])


class WStream:
    def __init__(self, kb, g8, tag, nbuf=2):
        self.kb = kb
        self.g8 = g8
        self.nbuf = nbuf
        self.wf = [kb.sb("wf%s%d" % (tag, i), [128, 8, 128]) for i in range(nbuf)]
        self.wb = [kb.sb("wb%s%d" % (tag, i), [128, 8, 128], BF16) for i in range(2)]
        self.i = 0

    def get(self, wsrc, col0, M):
        kb = self.kb
        wf = self.wf[self.i % self.nbuf]
        wb = self.wb[self.i % 2]
        self.i += 1
        kb.dma(wf[:, :, 0:M], wsrc[:, col0:col0 + M].rearrange("(kc p) n -> p kc n", p=128), writes=[wf])
        g8 = self.g8
        kb.op("pool", lambda h: h.tensor_tensor(out=wb[:, :, 0:M], in0=wf[:, :, 0:M],
                                                in1=g8[:].unsqueeze(2).to_broadcast([128, 8, M]), op=ALU.mult),
              reads=[wf, g8], writes=[wb])
        return wb


def proj_fm(kb, wb, M, hT, tiles, evac):
    for (t0, tn) in tiles:
        ps = kb.psf()
        for kc in range(8):
            kb.op("pe", lambda h, kc=kc, ps=ps, t0=t0, tn=tn: h.matmul(ps[0:M, 0:tn], lhsT=wb[:, kc, 0:M],
                                                                      rhs=hT[:, kc, t0:t0 + tn],
                                                                      start=(kc == 0), stop=(kc == 7)),
                  reads=[wb, hT], writes=[ps])
        evac(ps, t0, tn)


def tiles_of(t0, n, step=512):
    out = []
    t = t0
    while t < t0 + n:
        out.append((t, min(step, t0 + n - t)))
        t += step
    return out


def headnorm(kb, c, PT, t0, n, gain, out, tmp):
    sq, t1 = tmp
    for (a, tn) in tiles_of(t0, n):
        kb.op("act", lambda h, a=a, tn=tn: h.activation(out=sq[:, 0:tn], in_=PT[:, a:a + tn], func=AF.Square),
              reads=[PT], writes=[sq])
        ps = kb.psf()
        kb.op("pe", lambda h, ps=ps, tn=tn: h.matmul(ps[:, 0:tn], lhsT=c["blk64"][:], rhs=sq[:, 0:tn], start=True, stop=True),
              reads=[c["blk64"], sq], writes=[ps])
        kb.op("dve", lambda h, ps=ps, tn=tn: h.tensor_scalar(out=t1[:, 0:tn], in0=ps[:, 0:tn], scalar1=EPS, scalar2=None,
                                                             op0=ALU.add), reads=[ps], writes=[t1])
        kb.op("act", lambda h, tn=tn: h.activation(out=t1[:, 0:tn], in_=t1[:, 0:tn], func=AF.Sqrt), reads=[t1], writes=[t1])
        kb.op("dve", lambda h, tn=tn: h.reciprocal(out=t1[:, 0:tn], in_=t1[:, 0:tn]), reads=[t1], writes=[t1])
        kb.op("dve", lambda h, a=a, tn=tn: h.scalar_tensor_tensor(out=out[:, a:a + tn], in0=PT[:, a:a + tn], scalar=gain[:, 0:1],
                                                                  in1=t1[:, 0:tn], op0=ALU.mult, op1=ALU.mult),
              reads=[PT, gain, t1], writes=[out])


def build_LA():
    kb = KB()
    kb.init_psum()
    xe = kb.din("xe", [NE, D])
    wa = kb.din("wa", [D, 896])
    memx = kb.din("memx", [256, D])
    wm = kb.din("wm", [D, 512])
    out_a = kb.dout("out_a", [NT, 384])
    out_mT = kb.dout("out_mT", [256, NT])
    c = emit_consts(kb)
    g8 = kb.load("g8", kb.din("g8", [128, 8])[:, :], [128, 8])
    mg8 = kb.load("mg8", kb.din("mg8", [128, 8])[:, :], [128, 8])
    gains = kb.load("gains", kb.din("gains", [128, 4])[:, :], [128, 4])
    gsc = kb.sb("gsc", [128, 4])
    kb.op("dve", lambda h: h.tensor_scalar(out=gsc[:], in0=gains[:], scalar1=0.125, scalar2=None, op0=ALU.mult),
          reads=[gains], writes=[gsc])
    sink = kb.sb("sink", [128, 6])
    kb.dma(sink[:], kb.din("sinks", [1, 6]).partition_broadcast(128), writes=[sink])
    esink = kb.sb("esink", [128, 6])
    kb.op("act", lambda h: h.activation(out=esink[:], in_=sink[:], func=AF.Exp), reads=[sink], writes=[esink])
    biasT = kb.load("biasT", kb.din("biasT", [128, 6 * 256])[:, :], [128, 6 * 256])
    biasT0 = kb.load("biasT0", kb.din("biasT0", [128, 6 * 128])[:, :], [128, 6 * 128])
    onesb = kb.sb("onesb", [128, 64], BF16)
    kb.op("pool", lambda h: h.memset(onesb[:], 1.0), writes=[onesb])

    hT = kb.sb("hT", [128, 8, NE], BF16)
    norm_transpose(kb, c, xe, NE // 128, hT, "x")
    mT = kb.sb("mT", [128, 8, 256], BF16)
    norm_transpose(kb, c, memx, 2, mT, "m")

    PT = [kb.sb("PT%d" % i, [128, NE]) for i in range(4)]
    sq = kb.sb("hn_sq", [128, 512])
    t1 = kb.sb("hn_t1", [128, 512])
    ws = WStream(kb, g8, "a")
    wsm = WStream(kb, mg8, "m")
    all_tiles = [(0, 128)] + tiles_of(128, NT)

    def evac_to(dst):
        def f(ps, t0, tn):
            kb.op("act", lambda h: h.copy(out=dst[:, t0:t0 + tn], in_=ps[:, 0:tn]), reads=[ps], writes=[dst])
        return f

    knT = kb.sb("knT", [128, NE], BF16)
    wbk = ws.get(wa, 384, 128)
    proj_fm(kb, wbk, 128, hT, all_tiles, evac_to(PT[3]))
    headnorm(kb, c, PT[3], 0, NE, _col(kb, gains, 1), knT, (sq, t1))
    Vt = kb.sb("Vt", [128, 17, 2, 65], BF16)
    kb.op("pool", lambda h: h.memset(Vt[:], 1.0), writes=[Vt])
    wbv = ws.get(wa, 512, 128)
    for blk in range(17):
        ps = kb.psf()
        for kc in range(8):
            kb.op("pe", lambda h, kc=kc, ps=ps, blk=blk: h.matmul(ps[:, 0:128], lhsT=hT[:, kc, blk * 128:(blk + 1) * 128],
                                                                 rhs=wbv[:, kc, :], start=(kc == 0), stop=(kc == 7)),
                  reads=[wbv, hT], writes=[ps])
        kb.op("act", lambda h, ps=ps, blk=blk: h.copy(out=Vt[:, blk, :, 0:64],
                                                      in_=ps[:, 0:128].rearrange("p (g d) -> p g d", g=2)),
              reads=[ps], writes=[Vt])
    qnT = [kb.sb("qnT%d" % i, [128, NE], BF16) for i in range(3)]
    qg = _col(kb, gsc, 0)
    for cq in range(3):
        wbq = ws.get(wa, cq * 128, 128)
        proj_fm(kb, wbq, 128, hT, all_tiles[1:], evac_to(PT[cq]))
        headnorm(kb, c, PT[cq], 128, NT, qg, qnT[cq], (sq, t1))

    Pb = [kb.sb("Pb%d" % i, [128, 6, 256], BF16) for i in range(2)]
    lg = [kb.sb("lg%d" % i, [128, 256]) for i in range(2)]
    den = kb.sb("den", [128, 6, 1])
    oa = [kb.sb("oa%d" % i, [128, 6, 64]) for i in range(2)]
    li = 0
    for kblk in range(17):
        if kblk == 0:
            q0, qn, bsrc, boff, poff = 128, 128, biasT0, None, 128
        elif kblk == 16:
            q0, qn, bsrc, boff, poff = 16 * 128, 128, biasT, 0, 0
        else:
            q0, qn, bsrc, boff, poff = kblk * 128, 256, biasT, 0, 0
        P = Pb[kblk % 2]
        for hd in range(6):
            cq, half = hd % 3, hd // 3
            ps = kb.psf()
            kb.op("pe", lambda h, ps=ps, cq=cq, half=half, kblk=kblk, q0=q0, qn=qn: h.matmul(
                ps[:, 0:qn], lhsT=knT[64 * half:64 * half + 64, kblk * 128:(kblk + 1) * 128],
                rhs=qnT[cq][64 * half:64 * half + 64, q0:q0 + qn], start=True, stop=True),
                reads=[knT, qnT[cq]], writes=[ps])
            l = lg[li % 2]
            li += 1
            if kblk == 0:
                bap = biasT0[:, hd * 128:(hd + 1) * 128]
            else:
                bap = biasT[:, hd * 256:hd * 256 + qn]
            kb.op("dve", lambda h, ps=ps, l=l, bap=bap, qn=qn: h.tensor_tensor(out=l[:, 0:qn], in0=ps[:, 0:qn], in1=bap, op=ALU.add),
                  reads=[ps, bsrc], writes=[l])
            kb.op("act", lambda h, l=l, P=P, hd=hd, poff=poff, qn=qn: h.activation(out=P[:, hd, poff:poff + qn], in_=l[:, 0:qn], func=AF.Exp),
                  reads=[l], writes=[P])
        if kblk >= 1:
            Pp = Pb[(kblk - 1) % 2]
            po = kb.psf()
            po3 = po[:, 0:390].rearrange("p (h d) -> p h d", d=65)
            for hd in range(6):
                g = hd // 3
                kb.op("pe", lambda h, po=po, hd=hd, g=g, Pp=Pp, kblk=kblk: h.matmul(
                    po[:, hd * 65:(hd + 1) * 65], lhsT=Pp[:, hd, 128:256], rhs=Vt[:, kblk - 1, g, :], start=True, stop=False),
                    reads=[Pp, Vt], writes=[po])
                kb.op("pe", lambda h, po=po, hd=hd, g=g, P=P, kblk=kblk: h.matmul(
                    po[:, hd * 65:(hd + 1) * 65], lhsT=P[:, hd, 0:128], rhs=Vt[:, kblk, g, :], start=False, stop=True),
                    reads=[P, Vt], writes=[po])
            o = oa[kblk % 2]
            kb.op("dve", lambda h, po3=po3: h.tensor_tensor(out=den[:], in0=po3[:, :, 64:65], in1=esink[:].unsqueeze(2), op=ALU.add),
                  reads=[po, esink], writes=[den])
            kb.op("dve", lambda h: h.reciprocal(out=den[:], in_=den[:]), reads=[den], writes=[den])
            kb.op("dve", lambda h, po3=po3, o=o: h.tensor_tensor(out=o[:], in0=po3[:, :, 0:64], in1=den[:].to_broadcast([128, 6, 64]), op=ALU.mult),
                  reads=[po, den], writes=[o])
            kb.dma(out_a[(kblk - 1) * 128:kblk * 128, :].rearrange("p (h d) -> p h d", d=64), o[:], reads=[o])

    mknT = [kb.sb("mknT%d" % i, [128, 256], BF16) for i in range(2)]
    mkg = _col(kb, gains, 3)
    for cm in range(2):
        wbm = wsm.get(wm, cm * 128, 128)
        proj_fm(kb, wbm, 128, mT, [(0, 256)], evac_to(PT[3]))
        headnorm(kb, c, PT[3], 0, 256, mkg, mknT[cm], (sq, t1))
    mv = kb.sb("mv", [128, 2, 256], BF16)
    for half in range(2):
        wbm = wsm.get(wm, 256 + half * 128, 128)
        for mb in range(2):
            ps = kb.psf()
            for kc in range(8):
                kb.op("pe", lambda h, kc=kc, ps=ps, mb=mb, wbm=wbm: h.matmul(ps[:, 0:128], lhsT=mT[:, kc, mb * 128:(mb + 1) * 128],
                                                                            rhs=wbm[:, kc, :], start=(kc == 0), stop=(kc == 7)),
                      reads=[wbm, mT], writes=[ps])
            kb.op("act", lambda h, ps=ps, mb=mb, half=half: h.copy(out=mv[:, mb, half * 128:(half + 1) * 128], in_=ps[:, 0:128]),
                  reads=[ps], writes=[mv])
    mqg = _col(kb, gsc, 2)
    Pm = [kb.sb("Pm%d" % i, [128, 512], BF16) for i in range(2)]
    rd = kb.sb("rd", [128, 512])
    om = [kb.sb("om%d" % i, [128, 512]) for i in range(2)]
    for cm in range(2):
        wbq = ws.get(wa, 640 + cm * 128, 128)
        proj_fm(kb, wbq, 128, hT, all_tiles[1:], evac_to(PT[cm]))
        headnorm(kb, c, PT[cm], 128, NT, mqg, qnT[cm], (sq, t1))
        for ti, (t0, tn) in enumerate(all_tiles[1:]):
            po = kb.psf()
            pd = kb.psf()
            for half in range(2):
                hd = 2 * cm + half
                for mb in range(2):
                    ps = kb.psf()
                    kb.op("pe", lambda h, ps=ps, half=half, mb=mb, t0=t0, cm=cm: h.matmul(
                        ps[:, 0:512], lhsT=mknT[cm][64 * half:64 * half + 64, mb * 128:(mb + 1) * 128],
                        rhs=qnT[cm][64 * half:64 * half + 64, t0:t0 + 512], start=True, stop=True),
                        reads=[mknT[cm], qnT[cm]], writes=[ps])
                    kb.op("act", lambda h, ps=ps, mb=mb: h.activation(out=Pm[mb][:], in_=ps[:, 0:512], func=AF.Exp),
                          reads=[ps], writes=[Pm[mb]])
                for mb in range(2):
                    kb.op("pe", lambda h, po=po, half=half, mb=mb, hd=hd: h.matmul(
                        po[64 * half:64 * half + 64, 0:512], lhsT=mv[:, mb, hd * 64:(hd + 1) * 64], rhs=Pm[mb][:],
                        start=(mb == 0), stop=(mb == 1)), reads=[mv, Pm[mb]], writes=[po])
                for mb in range(2):
                    kb.op("pe", lambda h, pd=pd, half=half, mb=mb: h.matmul(
                        pd[64 * half:64 * half + 64, 0:512], lhsT=onesb[:, :], rhs=Pm[mb][:],
                        start=(mb == 0), stop=(mb == 1)), reads=[onesb, Pm[mb]], writes=[pd])
            o = om[ti % 2]
            kb.op("dve", lambda h, pd=pd: h.reciprocal(out=rd[:], in_=pd[:, 0:512]), reads=[pd], writes=[rd])
            kb.op("dve", lambda h, po=po, o=o: h.tensor_tensor(out=o[:], in0=po[:, 0:512], in1=rd[:], op=ALU.mult),
                  reads=[po, rd], writes=[o])
            kb.dma(out_mT[cm * 128:(cm + 1) * 128, t0 - 128:t0 - 128 + 512], o[:], reads=[o])
    print("LA", kb.S.emit())
    return kb.nc


class _ColT:
    def __init__(self, base, col):
        self.t = base.t
        self.b = base.b
        self.col = col

    def __getitem__(self, k):
        return self.t[:, self.col:self.col + 1]


def _col(kb, base, col):
    return _ColT(base, col)


def t5_bucket(dist):
    d = np.maximum(dist, 1).astype(np.float32)
    large = 16 + (np.log(d / np.float32(16)) / np.float32(math.log(128 / 16)) * np.float32(16)).astype(np.int32)
    large = np.minimum(large, 31)
    return np.where(dist < 16, dist, large)


def bias_tables(rel_bias):
    k = np.arange(128)[:, None]
    q = np.arange(128)[None, :]
    dcur = q - k
    dprev = q + 128 - k
    tab = np.full((128, 6, 256), NEG, np.float32)
    bc = rel_bias[t5_bucket(np.maximum(dcur, 0))]
    bp = rel_bias[t5_bucket(np.maximum(dprev, 0))]
    vc = (dcur >= 0)
    vp = (dprev < 128)
    for h in range(6):
        tab[:, h, 0:128] = np.where(vc, bc[:, :, h], NEG)
        tab[:, h, 128:256] = np.where(vp, bp[:, :, h], NEG)
    return tab


_PROGS = {}


def get_prog(name, builder):
    if name not in _PROGS:
        _PROGS[name] = builder()
    return _PROGS[name]


def run(nc, in_maps):
    res = run_bass_kernel_spmd(nc, in_maps, core_ids=list(range(NCORE)))
    return res.results


QPERM = [0, 3, 1, 4, 2, 5]


def host_LA_inputs(l, xcur, inp):
    consts = const_arrays()
    w = inp["w_in"][l]
    qcols = np.concatenate([np.arange(h * 64, (h + 1) * 64) for h in QPERM])
    wa = np.ascontiguousarray(np.concatenate([w[:, qcols], w[:, 384:512], w[:, 512:640], w[:, 1920:2176]], axis=1))
    g8 = np.ascontiguousarray(inp["mix_norm_g"][l].reshape(8, 128).T)
    mg8 = np.ascontiguousarray(inp["mem_norm_g"][l].reshape(8, 128).T)
    gains = np.stack([np.tile(inp["attn_q_norm"][l], 2), np.tile(inp["attn_k_norm"][l], 2),
                      np.tile(inp["mem_q_norm"][l], 2), np.tile(inp["mem_k_norm"][l], 2)], axis=1).astype(np.float32)
    tab = bias_tables(inp["rel_bias"])
    maps = []
    for c in range(NCORE):
        b, j = c // 4, c % 4
        xe = np.zeros((NE, D), np.float32)
        xe[128:] = xcur[b, j * NT:(j + 1) * NT]
        if j > 0:
            xe[:128] = xcur[b, j * NT - 128:j * NT]
            t0 = tab[:, :, 128:256]
        else:
            t0 = np.full((128, 6, 128), NEG, np.float32)
        m = dict(consts)
        m.update(xe=xe, wa=wa, memx=np.ascontiguousarray(inp["mem"][b]), wm=np.ascontiguousarray(inp["w_mem_kv"][l]),
                 g8=g8, mg8=mg8, gains=np.ascontiguousarray(gains), sinks=np.ascontiguousarray(inp["attn_sinks"][l][None, :]),
                 biasT=np.ascontiguousarray(tab.reshape(128, 6 * 256)), biasT0=np.ascontiguousarray(t0.reshape(128, 6 * 128)))
        maps.append(m)
    return maps


NV = 8


def scan_masks():
    i = np.arange(64)
    strict = (i[:, None] < i[None, :]).astype(np.float32)
    incl = (i[:, None] <= i[None, :]).astype(np.float32)
    z = np.zeros((64, 64), np.float32)

    def bd(m):
        return np.block([[m, z], [z, m]])
    mask4 = np.concatenate([bd(strict), bd(strict), bd(incl), bd(incl)], axis=1)
    maskL = bd(strict.T)
    swap = np.block([[z, np.eye(64, dtype=np.float32)], [np.eye(64, dtype=np.float32), z]])
    rmask = np.ones((128, 512), np.float32)
    rmask[:, ::64] = 0.0
    return dict(mask4=np.ascontiguousarray(mask4), maskL=np.ascontiguousarray(maskL), swap=np.ascontiguousarray(swap), rmask=rmask)


def build_LRS(layer1, stage=99):
    import os
    stage = int(os.environ.get("LRS_STAGE", stage))
    kb = KB()
    kb.init_psum()
    xe = kb.din("xe", [NE, D])
    wr = kb.din("wr", [D, 1280])
    yaug = kb.dout("yaug", [3, 128, 32 * 128])
    stT = kb.dout("stT", [3, 128, 128])
    bonusT = kb.dout("bonusT", [384, NT])
    gT = kb.dout("gT", [384, NT])
    vT = kb.dout("vT", [384, NT])
    c = emit_consts(kb)
    g8 = kb.load("g8", kb.din("g8", [128, 8])[:, :], [128, 8])
    mu10 = kb.load("mu10", kb.din("mu10", [128, 10])[:, :], [128, 10])
    chv = kb.load("chv", kb.din("chv", [128, 3 * NV])[:, :], [128, 3 * NV])
    lw = kb.load("lw", kb.din("lw", [128, 384])[:, :], [128, 384])
    rmask = kb.load("rmask", kb.din("rmask", [128, 512])[:, :], [128, 512])
    mask4 = kb.load("mask4", kb.din("mask4", [128, 512])[:, :], [128, 512])
    maskL = kb.load("maskL", kb.din("maskL", [128, 128])[:, :], [128, 128])
    swap = kb.load("swap", kb.din("swap", [128, 128])[:, :], [128, 128])
    identf = c["identf"]
    blk1 = c["blk1"]
    omka = kb.sb("omka", [128, 3])
    for pc in range(3):
        kb.op("dve", lambda h, pc=pc: h.tensor_scalar(out=omka[:, pc:pc + 1], in0=chv[:, pc * NV + 3:pc * NV + 4], scalar1=-1.0, scalar2=1.0,
                                                      op0=ALU.mult, op1=ALU.add), reads=[chv], writes=[omka])
    if layer1:
        wvr = kb.din("wvr", [D, 16])
        muv = kb.load("muv", kb.din("muv", [16, 1])[:, :], [16, 1])
        v2t = kb.load("v2t", kb.din("v2t", [16, 384])[:, :], [16, 384])
        vfT = kb.din("vfT", [384, NT])

    Wk = [kb.sb("Wk%d" % i, [128, NT]) for i in range(5)]
    hT = kb.sb("hT", [128, 8, NE], BF16)
    norm_transpose(kb, c, xe, NE // 128, hT, "x", scratch=([Wk[0], Wk[1]], Wk[2]))
    PTr = kb.sb("PTr", [128, NE])
    PTk = kb.sb("PTk", [128, NE])
    PTv = kb.sb("PTv", [128, NE])
    PTl = kb.sb("PTl", [128, NE])
    ws = WStream(kb, g8, "r", nbuf=1)
    all_tiles = [(0, 128)] + tiles_of(128, NT)
    own_tiles = tiles_of(128, NT)
    sig, a_, cs, tmp, kkn = Wk

    def evac_to(dst, rows=128):
        def f(ps, t0, tn):
            kb.op("act", lambda h: h.copy(out=dst[0:rows, t0:t0 + tn], in_=ps[0:rows, 0:tn]), reads=[ps], writes=[dst])
        return f

    def shift(PT, mu_t, mu_ap, rows=128):
        kb.op("dve", lambda h: h.tensor_tensor(out=tmp[0:rows, :], in0=PT[0:rows, 127:127 + NT], in1=PT[0:rows, 128:128 + NT], op=ALU.subtract),
              reads=[PT], writes=[tmp])
        kb.op("dve", lambda h: h.scalar_tensor_tensor(out=PT[0:rows, 128:128 + NT], in0=tmp[0:rows, :], scalar=mu_ap, in1=PT[0:rows, 128:128 + NT],
                                                      op0=ALU.mult, op1=ALU.add), reads=[tmp, mu_t, PT], writes=[PT])

    wb = ws.get(wr, 1152, 128)
    proj_fm(kb, wb, 128, hT, all_tiles, evac_to(PTl))
    shift(PTl, mu10, mu10[:, 9:10])
    kb.op("act", lambda h: h.activation(out=PTl[0:32, 128:], in_=PTl[0:32, 128:], func=AF.Tanh), reads=[PTl], writes=[PTl])
    kb.op("act", lambda h: h.activation(out=PTl[64:128, 128:], in_=PTl[64:128, 128:], func=AF.Sigmoid), reads=[PTl], writes=[PTl])
    if stage <= 1:
        print("LRS-stage1", kb.S.emit())
        return kb.nc
    if layer1:
        PT16 = kb.sb("PT16", [16, NE])
        wbv = ws.get(wvr, 0, 16)
        proj_fm(kb, wbv, 16, hT, all_tiles, evac_to(PT16, 16))
        shift(PT16, muv, muv[:, 0:1], rows=16)

    BD = {k: kb.sb("BD" + k, [128, 8, 128]) for k in "ARBKV"}
    for k in "ARBKV":
        kb.op("pool", lambda h, k=k: h.memset(BD[k][:], 0.0), writes=[BD[k]])
    E = [[kb.sb("E%d_%d" % (i, j), [128, 512]) for j in range(3)] for i in range(1)]
    x512 = kb.sb("x512", [128, 512])
    t512 = kb.sb("t512", [128, 512])
    s512, vf512 = t512, x512
    g512 = [kb.sb("g512_%d" % i, [128, 512]) for i in range(1)]
    b512 = [kb.sb("b512_%d" % i, [128, 512]) for i in range(1)]
    NR = 3
    TB = [kb.sb("TB%d" % i, [128, 384]) for i in range(NR)]
    LT = [kb.sb("LT%d" % i, [128, 512]) for i in range(NR)]
    X0 = [kb.sb("X0_%d" % i, [128, 128]) for i in range(NR)]
    XN = [kb.sb("XN%d" % i, [128, 256]) for i in range(6)]
    PM = [kb.sb("PM%d" % i, [128, 128]) for i in range(6)]
    TTb = [kb.sb("TTb%d" % i, [128, 128]) for i in range(NR)]
    Wb_ = [kb.sb("Wb%d" % i, [128, 128]) for i in range(2)]
    Ub_ = [kb.sb("Ub%d" % i, [128, 128]) for i in range(2)]
    tS = [kb.sb("tS%d" % i, [128, 128]) for i in range(2)]
    St = [kb.sb("St%d" % i, [128, 128]) for i in range(2)]
    Yst = [kb.sb("Yst%d" % i, [128, 8, 128]) for i in range(1)]
    cnt = dict(xn=0, pm=0, gi=0, st=0)

    def pre_gen(slot, ch):
        A, R, B, K, V = (BD[k] for k in "ARBKV")
        tb, lt, x0 = TB[slot], LT[slot], X0[slot]
        pT = kb.psf()
        for i, M in enumerate((B, K, V)):
            kb.op("pe", lambda h, i=i, M=M: h.transpose(out=pT[:, i * 128:(i + 1) * 128], in_=M[:, ch, :], identity=identf[:]),
                  reads=[M, identf], writes=[pT])
        kb.op("act", lambda h: h.copy(out=tb[:], in_=pT[:, 0:384]), reads=[pT], writes=[tb])
        pS = kb.psf()
        for i, (L_, R_) in enumerate(((B, A), (K, A), (B, R), (K, R))):
            kb.op("pe", lambda h, i=i, L_=L_, R_=R_: h.matmul(pS[:, i * 128:(i + 1) * 128], lhsT=L_[:, ch, :], rhs=R_[:, ch, :], start=True, stop=True),
                  reads=[L_, R_], writes=[pS])
        kb.op("dve", lambda h: h.tensor_tensor(out=lt[:], in0=pS[:, 0:512], in1=mask4[:], op=ALU.mult), reads=[pS, mask4], writes=[lt])
        pL = kb.psf()
        kb.op("pe", lambda h: h.matmul(pL[:, 0:128], lhsT=A[:, ch, :], rhs=B[:, ch, :], start=True, stop=True), reads=[A, B], writes=[pL])
        kb.op("dve", lambda h: h.tensor_tensor(out=x0[:], in0=pL[:, 0:128], in1=maskL[:], op=ALU.mult), reads=[pL, maskL], writes=[x0])
        pm = PM[cnt["pm"] % len(PM)]
        cnt["pm"] += 1
        kb.op("pool", lambda h: h.tensor_tensor(out=pm[:], in0=identf[:], in1=lt[:, 0:128], op=ALU.add), reads=[identf, lt], writes=[pm])
        yield
        X_t, X_ap, XT_t, XT_ap = x0, x0[:, :], lt, lt[:, 0:128]
        for j in range(5):
            xn = XN[cnt["xn"] % len(XN)]
            cnt["xn"] += 1
            pX = kb.psf()
            kb.op("pe", lambda h, XT_ap=XT_ap, X_ap=X_ap: h.matmul(pX[:, 0:128], lhsT=XT_ap, rhs=X_ap, start=True, stop=True),
                  reads=[X_t, XT_t], writes=[pX])
            ncol = 128
            if j < 4:
                kb.op("pe", lambda h, XT_ap=XT_ap, X_ap=X_ap: h.matmul(pX[:, 128:256], lhsT=X_ap, rhs=XT_ap, start=True, stop=True),
                      reads=[X_t, XT_t], writes=[pX])
                ncol = 256
            kb.op("act", lambda h, xn=xn, pX=pX, ncol=ncol: h.copy(out=xn[:, 0:ncol], in_=pX[:, 0:ncol]), reads=[pX], writes=[xn])
            yield
            pP = kb.psf()
            kb.op("pe", lambda h, xn=xn, pm=pm, pP=pP: h.matmul(pP[:, 0:128], lhsT=xn[:, 0:128], rhs=pm[:], start=True, stop=True),
                  reads=[xn, pm], writes=[pP])
            if j == 4:
                pm2 = TTb[slot]
            else:
                pm2 = PM[cnt["pm"] % len(PM)]
                cnt["pm"] += 1
            kb.op("dve", lambda h, pm=pm, pm2=pm2, pP=pP: h.tensor_tensor(out=pm2[:], in0=pP[:, 0:128], in1=pm[:], op=ALU.add),
                  reads=[pP, pm], writes=[pm2])
            pm = pm2
            X_t, X_ap, XT_t, XT_ap = xn, xn[:, 0:128], xn, xn[:, 128:256]
            yield
        pre_res[slot] = pm

    pre_res = {}

    def state_gen(slot, ch, gam_t, gam_ap, yst):
        A, R = BD["A"], BD["R"]
        tb, lt = TB[slot], LT[slot]
        tt = pre_res[slot]
        st_in = St[cnt["st"] % 2]
        st_out = St[(cnt["st"] + 1) % 2]
        cnt["st"] += 1
        wb_, ub_, ts_ = Wb_[ch % 2], Ub_[ch % 2], tS[ch % 2]
        pW = kb.psf()
        kb.op("pe", lambda h: h.matmul(pW[:, 0:128], lhsT=A[:, ch, :], rhs=st_in[:], start=True, stop=False), reads=[A, st_in], writes=[pW])
        kb.op("pe", lambda h: h.matmul(pW[:, 0:128], lhsT=lt[:, 128:256], rhs=tb[:, 256:384], start=False, stop=True), reads=[lt, tb], writes=[pW])
        kb.op("act", lambda h: h.copy(out=wb_[:], in_=pW[:, 0:128]), reads=[pW], writes=[wb_])
        yield
        pU = kb.psf()
        kb.op("pe", lambda h: h.matmul(pU[:, 0:128], lhsT=tt[:], rhs=wb_[:], start=True, stop=True), reads=[tt, wb_], writes=[pU])
        kb.op("dve", lambda h: h.tensor_copy(ub_[:], pU[:, 0:128]), reads=[pU], writes=[ub_])
        yield
        pN = kb.psf()
        kb.op("pe", lambda h: h.matmul(pN[:, 0:128], lhsT=tb[:, 0:128], rhs=ub_[:], start=True, stop=False), reads=[tb, ub_], writes=[pN])
        kb.op("pe", lambda h: h.matmul(pN[:, 0:128], lhsT=tb[:, 128:256], rhs=tb[:, 256:384], start=False, stop=True), reads=[tb], writes=[pN])
        kb.op("dve", lambda h: h.tensor_tensor(out=ts_[:], in0=pN[:, 0:128], in1=st_in[:], op=ALU.add), reads=[pN, st_in], writes=[ts_])
        kb.op("dve", lambda h: h.tensor_scalar(out=st_out[:], in0=ts_[:], scalar1=gam_ap, scalar2=None, op0=ALU.mult),
              reads=[ts_, gam_t], writes=[st_out])
        pY = kb.psf()
        kb.op("pe", lambda h: h.matmul(pY[:, 0:128], lhsT=st_in[:], rhs=R[:, ch, :], start=True, stop=False), reads=[st_in, R], writes=[pY])
        kb.op("pe", lambda h: h.matmul(pY[:, 0:128], lhsT=ub_[:], rhs=lt[:, 256:384], start=False, stop=False), reads=[ub_, lt], writes=[pY])
        kb.op("pe", lambda h: h.matmul(pY[:, 0:128], lhsT=tb[:, 256:384], rhs=lt[:, 384:512], start=False, stop=True), reads=[tb, lt], writes=[pY])
        kb.op("act", lambda h: h.copy(out=yst[:, ch, :], in_=pY[:, 0:128]), reads=[pY], writes=[yst])
        yield

    sub = int(os.environ.get("LRS_SUB", 10 ** 9))
    steps = [0]

    class StopBuild(Exception):
        pass

    def drive(gens):
        gens = list(gens)
        while gens:
            for g in list(gens):
                try:
                    next(g)
                except StopIteration:
                    gens.remove(g)
                steps[0] += 1
                if steps[0] >= sub:
                    raise StopBuild()

    def v3(ap2):
        return ap2.rearrange("p (c t) -> p c t", t=64)

    def _scan_pair(pc):
        def col(i):
            return chv[:, pc * NV + i:pc * NV + i + 1]
        for (PT, col0, mc) in ((PTr, pc * 128, pc), (PTk, 384 + pc * 128, 3 + pc), (PTv, 768 + pc * 128, 6 + pc)):
            wb = ws.get(wr, col0, 128)
            proj_fm(kb, wb, 128, hT, all_tiles, evac_to(PT))
            shift(PT, mu10, mu10[:, mc:mc + 1])
        for ti, (t0, tn) in enumerate(own_tiles):
            o = t0 - 128
            ps = kb.psf()
            kb.op("pe", lambda h, ps=ps, t0=t0: h.matmul(ps[:, 0:512], lhsT=lw[0:32, pc * 128:(pc + 1) * 128], rhs=PTl[0:32, t0:t0 + 512], start=True, stop=True),
                  reads=[lw, PTl], writes=[ps])
            kb.op("act", lambda h, ps=ps, o=o: h.activation(out=sig[:, o:o + 512], in_=ps[:, 0:512], func=AF.Sigmoid, bias=col(0)),
                  reads=[ps, chv], writes=[sig])
            ps = kb.psf()
            kb.op("pe", lambda h, ps=ps, t0=t0: h.matmul(ps[:, 0:512], lhsT=lw[32:64, pc * 128:(pc + 1) * 128], rhs=PTl[32:64, t0:t0 + 512], start=True, stop=True),
                  reads=[lw, PTl], writes=[ps])
            kb.op("act", lambda h, ps=ps, o=o: h.activation(out=a_[:, o:o + 512], in_=ps[:, 0:512], func=AF.Sigmoid, bias=col(1)),
                  reads=[ps, chv], writes=[a_])
            ps = kb.psf()
            kb.op("pe", lambda h, ps=ps, t0=t0: h.matmul(ps[:, 0:512], lhsT=lw[64:128, pc * 128:(pc + 1) * 128], rhs=PTl[64:128, t0:t0 + 512], start=True, stop=True),
                  reads=[lw, PTl], writes=[ps])
            gs = g512[0]
            kb.op("act", lambda h, ps=ps, gs=gs: h.copy(out=gs[:], in_=ps[:, 0:512]), reads=[ps], writes=[gs])
            kb.dma(gT[pc * 128:(pc + 1) * 128, o:o + 512], gs[:], reads=[gs])
            if layer1:
                ps = kb.psf()
                kb.op("pe", lambda h, ps=ps, t0=t0: h.matmul(ps[:, 0:512], lhsT=v2t[0:16, pc * 128:(pc + 1) * 128], rhs=PT16[0:16, t0:t0 + 512], start=True, stop=True),
                      reads=[v2t, PT16], writes=[ps])
                kb.op("act", lambda h, ps=ps: h.activation(out=s512[:], in_=ps[:, 0:512], func=AF.Sigmoid, bias=col(5)),
                      reads=[ps, chv], writes=[s512])
                kb.dma(vf512[:], vfT[pc * 128:(pc + 1) * 128, o:o + 512], writes=[vf512])
                kb.op("dve", lambda h, t0=t0: h.tensor_tensor(out=vf512[:], in0=vf512[:], in1=PTv[:, t0:t0 + 512], op=ALU.subtract),
                      reads=[vf512, PTv], writes=[vf512])
                kb.op("dve", lambda h: h.tensor_tensor(out=vf512[:], in0=vf512[:], in1=s512[:], op=ALU.mult), reads=[vf512, s512], writes=[vf512])
                kb.op("dve", lambda h, t0=t0: h.tensor_tensor(out=PTv[:, t0:t0 + 512], in0=PTv[:, t0:t0 + 512], in1=vf512[:], op=ALU.add),
                      reads=[vf512, PTv], writes=[PTv])
        kb.dma(vT[pc * 128:(pc + 1) * 128, :], PTv[:, 128:128 + NT], reads=[PTv])
        kb.op("dve", lambda h: h.tensor_scalar(out=kkn[:], in0=PTk[:, 128:128 + NT], scalar1=col(2), scalar2=None, op0=ALU.mult),
              reads=[PTk, chv], writes=[kkn])
        kb.op("act", lambda h: h.activation(out=tmp[:], in_=kkn[:], func=AF.Square), reads=[kkn], writes=[tmp])
        for (t0, tn) in own_tiles:
            o = t0 - 128
            ps = kb.psf()
            kb.op("pe", lambda h, ps=ps, o=o: h.matmul(ps[:, 0:512], lhsT=blk1[:], rhs=tmp[:, o:o + 512], start=True, stop=True),
                  reads=[blk1, tmp], writes=[ps])
            kb.op("act", lambda h, ps=ps: h.activation(out=t512[:], in_=ps[:, 0:512], func=AF.Sqrt), reads=[ps], writes=[t512])
            kb.op("dve", lambda h: h.tensor_scalar(out=t512[:], in0=t512[:], scalar1=1e-12, scalar2=None, op0=ALU.max), reads=[t512], writes=[t512])
            kb.op("dve", lambda h: h.reciprocal(out=t512[:], in_=t512[:]), reads=[t512], writes=[t512])
            kb.op("dve", lambda h, o=o: h.tensor_tensor(out=kkn[:, o:o + 512], in0=kkn[:, o:o + 512], in1=t512[:], op=ALU.mult),
                  reads=[kkn, t512], writes=[kkn])
        kb.op("dve", lambda h: h.tensor_scalar(out=tmp[:], in0=a_[:], scalar1=col(3), scalar2=omka[:, pc:pc + 1], op0=ALU.mult, op1=ALU.add),
              reads=[a_, chv, omka], writes=[tmp])
        kb.op("dve", lambda h: h.tensor_tensor(out=PTk[:, 128:128 + NT], in0=PTk[:, 128:128 + NT], in1=tmp[:], op=ALU.mult),
              reads=[PTk, tmp], writes=[PTk])
        kb.op("dve", lambda h: h.scalar_tensor_tensor(out=tmp[:], in0=PTr[:, 128:128 + NT], scalar=col(4), in1=PTk[:, 128:128 + NT],
                                                      op0=ALU.mult, op1=ALU.mult), reads=[PTr, PTk, chv], writes=[tmp])
        for ti, (t0, tn) in enumerate(own_tiles):
            o = t0 - 128
            ps = kb.psf()
            kb.op("pe", lambda h, ps=ps, o=o: h.matmul(ps[:, 0:512], lhsT=blk1[:], rhs=tmp[:, o:o + 512], start=True, stop=True),
                  reads=[blk1, tmp], writes=[ps])
            bs = b512[0]
            kb.op("dve", lambda h, ps=ps, bs=bs, t0=t0: h.tensor_tensor(out=bs[:], in0=ps[:, 0:512], in1=PTv[:, t0:t0 + 512], op=ALU.mult),
                  reads=[ps, PTv], writes=[bs])
            kb.dma(bonusT[pc * 128:(pc + 1) * 128, o:o + 512], bs[:], reads=[bs])
        for g in range(4):
            kb.op("dve", lambda h, g=g: h.tensor_tensor_scan(out=cs[:, g * 512:(g + 1) * 512], data0=rmask[:], data1=sig[:, g * 512:(g + 1) * 512],
                                                            initial=0.0, op0=ALU.mult, op1=ALU.add), reads=[rmask, sig], writes=[cs])
        kb.op("dve", lambda h: h.tensor_tensor(out=a_[:], in0=a_[:], in1=kkn[:], op=ALU.mult), reads=[a_, kkn], writes=[a_])
        if stage <= 2:
            print("LRS-stage2", kb.S.emit())
            return True
        st0 = St[cnt["st"] % 2]
        kb.op("pool", lambda h, st0=st0: h.tensor_copy(st0[:], swap[:]), reads=[swap], writes=[st0])
        for g in range(4):
            o = g * 512
            Ei, Ev, Ex = E[0]
            kb.op("act", lambda h, o=o, Ei=Ei: h.activation(out=Ei[:], in_=cs[:, o:o + 512], func=AF.Exp, scale=-C0), reads=[cs], writes=[Ei])
            kb.op("act", lambda h, o=o, Ev=Ev: h.activation(out=Ev[:], in_=cs[:, o:o + 512], func=AF.Exp, scale=C0), reads=[cs], writes=[Ev])
            kb.op("dve", lambda h, o=o: h.tensor_tensor(out=x512[:], in0=cs[:, o:o + 512], in1=sig[:, o:o + 512], op=ALU.subtract),
                  reads=[cs, sig], writes=[x512])
            kb.op("act", lambda h, Ex=Ex: h.activation(out=Ex[:], in_=x512[:], func=AF.Exp, scale=-C0), reads=[x512], writes=[Ex])
            for hf in range(2):
                r = slice(64 * hf, 64 * hf + 64)
                cc = slice(64 * hf, 64 * hf + 64)
                kb.op("dve", lambda h, r=r, cc=cc, o=o, Ex=Ex: h.scalar_tensor_tensor(out=BD["A"][r, :, cc], in0=v3(kkn[r, o:o + 512]), scalar=-1.0,
                                                                                     in1=v3(Ex[r, :]), op0=ALU.mult, op1=ALU.mult),
                      reads=[kkn, Ex], writes=[BD["A"]])
                kb.op("pool", lambda h, r=r, cc=cc, o=o, Ei=Ei: h.tensor_tensor(out=BD["R"][r, :, cc], in0=v3(PTr[r, 128 + o:128 + o + 512]), in1=v3(Ei[r, :]), op=ALU.mult),
                      reads=[PTr, Ei], writes=[BD["R"]])
                kb.op("dve", lambda h, r=r, cc=cc, o=o, Ev=Ev: h.tensor_tensor(out=BD["B"][r, :, cc], in0=v3(a_[r, o:o + 512]), in1=v3(Ev[r, :]), op=ALU.mult),
                      reads=[a_, Ev], writes=[BD["B"]])
                kb.op("pool", lambda h, r=r, cc=cc, o=o, Ev=Ev: h.tensor_tensor(out=BD["K"][r, :, cc], in0=v3(PTk[r, 128 + o:128 + o + 512]), in1=v3(Ev[r, :]), op=ALU.mult),
                      reads=[PTk, Ev], writes=[BD["K"]])
                kb.op("act", lambda h, r=r, cc=cc, o=o: h.copy(out=BD["V"][r, :, cc], in_=v3(PTv[r, 128 + o:128 + o + 512])),
                      reads=[PTv], writes=[BD["V"]])
            if stage <= 3:
                print("LRS-stage3", kb.S.emit())
                return True
            yst = Yst[0]
            DEPTH = int(os.environ.get("LRS_DEPTH", 3))
            pending = {}
            started = set()
            for ch in range(8):
                if ch not in started:
                    started.add(ch)
                    pending[ch] = pre_gen((cnt["gi"] + ch) % NR, ch)
                if ch in pending:
                    drive([pending.pop(ch)])
                for k in range(ch + 1, min(ch + DEPTH, 8)):
                    if k not in started:
                        started.add(k)
                        pending[k] = pre_gen((cnt["gi"] + k) % NR, k)
                col_end = ch * 64 + 63
                sg = state_gen((cnt["gi"] + ch) % NR, ch, Ei, Ei[:, col_end:col_end + 1], yst)
                done = False
                while not done:
                    try:
                        next(sg)
                    except StopIteration:
                        done = True
                    for k in sorted(pending):
                        try:
                            next(pending[k])
                        except StopIteration:
                            del pending[k]
            cnt["gi"] += 8
            if stage <= 4:
                print("LRS-stage4", kb.S.emit())
                return True
            kb.dma(yaug[pc, :, g * 1024:(g + 1) * 1024], yst[:].rearrange("p c t -> p (c t)"), reads=[yst])
        stf = St[cnt["st"] % 2]
        kb.dma(stT[pc, :, :], stf[:], reads=[stf])
        return False

    try:
        for pc in range(3):
            if _scan_pair(pc):
                return kb.nc
    except StopBuild:
        print("LRS-sub", kb.S.emit())
        return kb.nc
    print("LRS", kb.S.emit())
    return kb.nc


GN_EPS = 64e-5


def build_LB(debug=False):
    kb = KB()
    kb.init_psum()
    yaug = kb.din("yaug", [3, 128, 32 * 128])
    epred = kb.din("epred", [3, 3, 128, 128])
    bonusT = kb.din("bonusT", [384, NT])
    gT = kb.din("gT", [384, NT])
    oaT = kb.din("oaT", [384, NT])
    omT = kb.din("omT", [256, NT])
    wout = kb.din("wout", [D, D])
    xown = kb.din("xown", [NT, D])
    xmid = kb.dout("xmid", [NT, D])
    if debug:
        outbT = kb.dout("outbT", [384, NT])
    c = emit_consts(kb, need_bf_ident=False)
    identf, blk1, blk64 = c["identf"], c["blk1"], c["blk64"]
    chv = kb.load("chv", kb.din("chv", [128, 3 * NV])[:, :], [128, 3 * NV])
    swap = kb.load("swap", kb.din("swap", [128, 128])[:, :], [128, 128])
    Gf = []
    for pc in range(3):
        stS = kb.sb("stS%d" % pc, [128, 128])
        kb.op("pool", lambda h, stS=stS: h.memset(stS[:], 0.0), writes=[stS])
        Ei = kb.sb("Ei%d" % pc, [128, 128])
        Q = kb.sb("Q%d" % pc, [128, 128])
        EiT = kb.sb("EiT%d" % pc, [128, 128])
        tq = kb.sb("tq%d" % pc, [128, 128])
        for slot in range(3):
            kb.dma(Ei[:], epred[slot, pc, :, :], writes=[Ei])
            pQ = kb.psf()
            kb.op("pe", lambda h, pQ=pQ, stS=stS: h.matmul(pQ[:, 0:128], lhsT=swap[:], rhs=stS[:], start=True, stop=True), reads=[swap, stS], writes=[pQ])
            kb.op("act", lambda h, pQ=pQ, Q=Q: h.copy(out=Q[:], in_=pQ[:, 0:128]), reads=[pQ], writes=[Q])
            pT = kb.psf()
            kb.op("pe", lambda h, pT=pT, Ei=Ei: h.transpose(out=pT[:, 0:128], in_=Ei[:], identity=identf[:]), reads=[Ei, identf], writes=[pT])
            kb.op("act", lambda h, pT=pT, EiT=EiT: h.copy(out=EiT[:], in_=pT[:, 0:128]), reads=[pT], writes=[EiT])
            pC = kb.psf()
            kb.op("pe", lambda h, pC=pC, EiT=EiT, Q=Q: h.matmul(pC[:, 0:128], lhsT=EiT[:], rhs=Q[:], start=True, stop=True), reads=[EiT, Q], writes=[pC])
            kb.op("dve", lambda h, pC=pC, Ei=Ei, tq=tq: h.tensor_tensor(out=tq[:], in0=pC[:, 0:128], in1=Ei[:], op=ALU.add), reads=[pC, Ei], writes=[tq])
            kb.op("dve", lambda h, tq=tq, stS=stS: h.tensor_tensor(out=stS[:], in0=tq[:], in1=blk1[:], op=ALU.mult), reads=[tq, blk1], writes=[stS])
        pQ = kb.psf()
        kb.op("pe", lambda h, pQ=pQ, stS=stS: h.matmul(pQ[:, 0:128], lhsT=swap[:], rhs=stS[:], start=True, stop=True), reads=[swap, stS], writes=[pQ])
        gf = kb.sb("Gf%d" % pc, [128, 128])
        kb.op("dve", lambda h, pQ=pQ, gf=gf: h.tensor_tensor(out=gf[:], in0=pQ[:, 0:128], in1=identf[:], op=ALU.add), reads=[pQ, identf], writes=[gf])
        Gf.append(gf)
    mixT = [kb.sb("mixT%d" % i, [128, NT], BF16) for i in range(8)]
    stg = [kb.sb("stg%d" % i, [128, 512]) for i in range(2)]
    si = 0
    for (src, nchunk, base) in ((oaT, 3, 0), (omT, 2, 6)):
        for cc in range(nchunk):
            for g in range(4):
                s_ = stg[si % 2]
                si += 1
                kb.dma(s_[:], src[cc * 128:(cc + 1) * 128, g * 512:(g + 1) * 512], writes=[s_])
                kb.op("pool", lambda h, s_=s_, cc=cc, g=g, base=base: h.tensor_copy(mixT[base + cc][:, g * 512:(g + 1) * 512], s_[:]),
                      reads=[s_], writes=[mixT[base + cc]])
    ya = [kb.sb("ya%d" % i, [128, 8, 128]) for i in range(2)]
    ysb = kb.sb("ysb", [128, 512])
    yc = kb.sb("yc", [128, 512])
    sq = kb.sb("sq", [128, 512])
    tt = kb.sb("tt", [128, 512])
    bt = [kb.sb("bt%d" % i, [128, 512]) for i in range(2)]
    gt = [kb.sb("gt%d" % i, [128, 512]) for i in range(2)]
    if debug:
        dbg = [kb.sb("dbg%d" % i, [128, 512]) for i in range(2)]
    it = 0
    for pc in range(3):
        lnw = chv[:, pc * NV + 6:pc * NV + 7]
        lnb = chv[:, pc * NV + 7:pc * NV + 8]
        for g in range(4):
            y_ = ya[it % 2]
            b_ = bt[it % 2]
            g_ = gt[it % 2]
            kb.dma(y_[:].rearrange("p c t -> p (c t)"), yaug[pc, :, g * 1024:(g + 1) * 1024], writes=[y_])
            kb.dma(b_[:], bonusT[pc * 128:(pc + 1) * 128, g * 512:(g + 1) * 512], writes=[b_])
            kb.dma(g_[:], gT[pc * 128:(pc + 1) * 128, g * 512:(g + 1) * 512], writes=[g_])
            py = kb.psf()
            for hf in range(2):
                kb.op("pe", lambda h, py=py, hf=hf, y_=y_, pc=pc: h.matmul(
                    py[64 * hf:64 * hf + 64, 0:512].rearrange("p (c t) -> p c t", t=64), lhsT=Gf[pc][:, 64 * hf:64 * hf + 64],
                    rhs=y_[:, :, 64 * hf:64 * hf + 64], start=True, stop=True), reads=[Gf[pc], y_], writes=[py])
            kb.op("act", lambda h, py=py: h.copy(out=ysb[:], in_=py[:, 0:512]), reads=[py], writes=[ysb])
            pm = kb.psf()
            kb.op("pe", lambda h, pm=pm: h.matmul(pm[:, 0:512], lhsT=blk64[:], rhs=ysb[:], start=True, stop=True), reads=[blk64, ysb], writes=[pm])
            kb.op("dve", lambda h, pm=pm: h.tensor_tensor(out=yc[:], in0=ysb[:], in1=pm[:, 0:512], op=ALU.subtract), reads=[ysb, pm], writes=[yc])
            kb.op("act", lambda h: h.activation(out=sq[:], in_=yc[:], func=AF.Square), reads=[yc], writes=[sq])
            pv = kb.psf()
            kb.op("pe", lambda h, pv=pv: h.matmul(pv[:, 0:512], lhsT=blk64[:], rhs=sq[:], start=True, stop=True), reads=[blk64, sq], writes=[pv])
            kb.op("dve", lambda h, pv=pv: h.tensor_scalar(out=tt[:], in0=pv[:, 0:512], scalar1=GN_EPS, scalar2=None, op0=ALU.add), reads=[pv], writes=[tt])
            kb.op("act", lambda h: h.activation(out=tt[:], in_=tt[:], func=AF.Sqrt), reads=[tt], writes=[tt])
            kb.op("dve", lambda h: h.reciprocal(out=tt[:], in_=tt[:]), reads=[tt], writes=[tt])
            kb.op("dve", lambda h: h.tensor_tensor(out=yc[:], in0=yc[:], in1=tt[:], op=ALU.mult), reads=[yc, tt], writes=[yc])
            kb.op("dve", lambda h, lnw=lnw, lnb=lnb: h.tensor_scalar(out=yc[:], in0=yc[:], scalar1=lnw, scalar2=lnb, op0=ALU.mult, op1=ALU.add),
                  reads=[yc, chv], writes=[yc])
            kb.op("dve", lambda h, b_=b_: h.tensor_tensor(out=yc[:], in0=yc[:], in1=b_[:], op=ALU.add), reads=[yc, b_], writes=[yc])
            if debug:
                d_ = dbg[it % 2]
                kb.op("dve", lambda h, g_=g_, d_=d_: h.tensor_tensor(out=d_[:], in0=yc[:], in1=g_[:], op=ALU.mult), reads=[yc, g_], writes=[d_])
                kb.dma(outbT[pc * 128:(pc + 1) * 128, g * 512:(g + 1) * 512], d_[:], reads=[d_])
            kb.op("dve", lambda h, g_=g_, pc=pc, g=g: h.tensor_tensor(out=mixT[3 + pc][:, g * 512:(g + 1) * 512], in0=yc[:], in1=g_[:], op=ALU.mult),
                  reads=[yc, g_], writes=[mixT[3 + pc]])
            it += 1
    Wo = kb.sb("Wo", [128, 8, D], BF16)
    wst = [kb.sb("wst%d" % i, [128, D]) for i in range(2)]
    for kc in range(8):
        w_ = wst[kc % 2]
        kb.dma(w_[:], wout[kc * 128:(kc + 1) * 128, :], writes=[w_])
        kb.op("act", lambda h, w_=w_, kc=kc: h.copy(out=Wo[:, kc, :], in_=w_[:]), reads=[w_], writes=[Wo])
    xt = [kb.sb("xt%d" % i, [128, D]) for i in range(2)]
    for blk in range(NB):
        x_ = xt[blk % 2]
        kb.dma(x_[:], xown[blk * 128:(blk + 1) * 128, :], writes=[x_])
        for half in range(2):
            ps = kb.psf()
            for kc in range(8):
                kb.op("pe", lambda h, ps=ps, kc=kc, blk=blk, half=half: h.matmul(ps[:, 0:512], lhsT=mixT[kc][:, blk * 128:(blk + 1) * 128],
                                                                                rhs=Wo[:, kc, half * 512:(half + 1) * 512], start=(kc == 0), stop=(kc == 7)),
                      reads=[mixT[kc], Wo], writes=[ps])
            kb.op("dve", lambda h, ps=ps, x_=x_, half=half: h.tensor_tensor(out=x_[:, half * 512:(half + 1) * 512], in0=x_[:, half * 512:(half + 1) * 512],
                                                                            in1=ps[:, 0:512], op=ALU.add), reads=[ps, x_], writes=[x_])
        kb.dma(xmid[blk * 128:(blk + 1) * 128, :], x_[:], reads=[x_])
    print("LB", kb.S.emit())
    return kb.nc


def host_LRS_inputs(l, xcur, inp, vfirstT=None):
    consts = const_arrays()
    consts.update(scan_masks())
    w = inp["w_in"][l]
    wr = np.ascontiguousarray(w[:, 640:1920])
    g8 = np.ascontiguousarray(inp["mix_norm_g"][l].reshape(8, 128).T)
    mu10 = np.ascontiguousarray(inp["rwkv_mu"][l].reshape(10, 128).T)
    chv = host_chv(l, inp)
    lw = np.ascontiguousarray(np.concatenate([inp["rwkv_w2"][l], inp["rwkv_a2"][l], inp["rwkv_g2"][l]], axis=0))
    maps = []
    for c in range(NCORE):
        b, j = c // 4, c % 4
        xe = np.zeros((NE, D), np.float32)
        xe[128:] = xcur[b, j * NT:(j + 1) * NT]
        if j > 0:
            xe[:128] = xcur[b, j * NT - 128:j * NT]
        m = dict(consts)
        m.pop("blk64", None)
        m["blk64"] = consts["blk64"]
        m.update(xe=xe, wr=wr, g8=g8, mu10=mu10, chv=chv, lw=lw)
        if l > 0:
            m.update(wvr=np.ascontiguousarray(inp["w_in_vres"][l - 1]), muv=np.ascontiguousarray(inp["rwkv_mu_vres"][l - 1][:, None]),
                     v2t=np.ascontiguousarray(inp["rwkv_v2"][l - 1]), vfT=np.ascontiguousarray(vfirstT[c]))
        maps.append(m)
    return maps


def host_chv(l, inp):
    vecs = [inp["rwkv_w0"][l], inp["rwkv_a0"][l], inp["rwkv_k_k"][l], inp["rwkv_k_a"][l], inp["rwkv_r_k"][l].reshape(384),
            inp["rwkv_v0"][l - 1] if l > 0 else np.zeros(384, np.float32), inp["rwkv_ln_w"][l], inp["rwkv_ln_b"][l]]
    chv = np.zeros((128, 3 * NV), np.float32)
    for i, v in enumerate(vecs):
        for pc in range(3):
            chv[:, pc * NV + i] = v[pc * 128:(pc + 1) * 128]
    return chv


def host_LB_inputs(l, xcur, inp, rs, la):
    consts = const_arrays()
    sm = scan_masks()
    chv = host_chv(l, inp)
    maps = []
    for c in range(NCORE):
        b, j = c // 4, c % 4
        ep = np.zeros((3, 3, 128, 128), np.float32)
        for slot in range(3):
            if slot < j:
                ep[slot] = rs[b * 4 + slot]["stT"]
            else:
                ep[slot] = sm["swap"][None]
        m = dict(consts)
        m.update(yaug=rs[c]["yaug"], epred=ep, bonusT=rs[c]["bonusT"], gT=rs[c]["gT"],
                 oaT=np.ascontiguousarray(la[c]["out_a"].T), omT=la[c]["out_mT"], wout=np.ascontiguousarray(inp["w_out"][l]),
                 xown=np.ascontiguousarray(xcur[b, j * NT:(j + 1) * NT]), chv=chv, swap=sm["swap"])
        maps.append(m)
    return maps


def build_LF():
    kb = KB()
    kb.init_psum()
    xme = kb.din("xme", [NE, D])
    wup = kb.din("wup", [D, 5632])
    wdn = kb.din("wdn", [2816, D])
    xnext = kb.dout("xnext", [NT, D])
    c = emit_consts(kb)
    g8 = kb.load("g8", kb.din("g8", [128, 8])[:, :], [128, 8])
    cw = kb.load("cw", kb.din("cw", [128, 44 * 4])[:, :], [128, 44 * 4])
    U = [kb.sb("U%d" % i, [128, NE]) for i in range(2)]
    junk = kb.sb("junkf", [128, D])
    hT = kb.sb("hT", [128, 8, NE], BF16)
    norm_transpose(kb, c, xme, NE // 128, hT, "x", scratch=([U[0], U[1]], junk))
    ws = WStream(kb, g8, "u")
    actT = [kb.sb("actT%d" % i, [128, NT], BF16) for i in range(22)]
    cb = [[kb.sb("cb%d_%d" % (i, j), [128, 512]) for j in range(2)] for i in range(2)]
    all_tiles = [(0, 128)] + tiles_of(128, NT)
    own_tiles = tiles_of(128, NT)

    def evac_to(dst):
        def f(ps, t0, tn):
            kb.op("act", lambda h: h.copy(out=dst[:, t0:t0 + tn], in_=ps[:, 0:tn]), reads=[ps], writes=[dst])
        return f

    for gc in range(22):
        for which, cchunk in enumerate((gc, 22 + gc)):
            wb = ws.get(wup, cchunk * 128, 128)
            proj_fm(kb, wb, 128, hT, all_tiles, evac_to(U[which]))
        for ti, (t0, tn) in enumerate(own_tiles):
            o = t0 - 128
            cg, cv = cb[ti % 2]
            for which, cchunk, dst in ((0, gc, cg), (1, 22 + gc, cv)):
                u = U[which]
                b4 = cchunk * 4
                kb.op("pool", lambda h: h.tensor_scalar(out=dst[:], in0=u[:, t0:t0 + 512], scalar1=cw[:, b4 + 2:b4 + 3], scalar2=cw[:, b4 + 3:b4 + 4],
                                                        op0=ALU.mult, op1=ALU.add), reads=[u, cw], writes=[dst])
                kb.op("dve", lambda h: h.scalar_tensor_tensor(out=dst[:], in0=u[:, t0 - 1:t0 - 1 + 512], scalar=cw[:, b4 + 1:b4 + 2], in1=dst[:],
                                                              op0=ALU.mult, op1=ALU.add), reads=[u, cw, dst], writes=[dst])
                kb.op("dve", lambda h: h.scalar_tensor_tensor(out=dst[:], in0=u[:, t0 - 2:t0 - 2 + 512], scalar=cw[:, b4:b4 + 1], in1=dst[:],
                                                              op0=ALU.mult, op1=ALU.add), reads=[u, cw, dst], writes=[dst])
            kb.op("act", lambda h: h.activation(out=cg[:], in_=cg[:], func=AF.Silu), reads=[cg], writes=[cg])
            kb.op("pool", lambda h: h.tensor_tensor(out=actT[gc][:, o:o + 512], in0=cg[:], in1=cv[:], op=ALU.mult), reads=[cg, cv], writes=[actT[gc]])
    Wd = kb.sb("Wd", [128, 22, 512], BF16)
    wst = [kb.sb("wst%d" % i, [128, 512]) for i in range(2)]
    xt = [kb.sb("xt%d" % i, [128, 512]) for i in range(2)]
    for half in range(2):
        for gc in range(22):
            w_ = wst[gc % 2]
            kb.dma(w_[:], wdn[gc * 128:(gc + 1) * 128, half * 512:(half + 1) * 512], writes=[w_])
            kb.op("act", lambda h: h.copy(out=Wd[:, gc, :], in_=w_[:]), reads=[w_], writes=[Wd])
        for blk in range(NB):
            x_ = xt[blk % 2]
            kb.dma(x_[:], xme[128 + blk * 128:128 + (blk + 1) * 128, half * 512:(half + 1) * 512], writes=[x_])
            ps = kb.psf()
            for gc in range(22):
                kb.op("pe", lambda h: h.matmul(ps[:, 0:512], lhsT=actT[gc][:, blk * 128:(blk + 1) * 128], rhs=Wd[:, gc, :],
                                               start=(gc == 0), stop=(gc == 21)), reads=[actT[gc], Wd], writes=[ps])
            kb.op("dve", lambda h: h.tensor_tensor(out=x_[:], in0=x_[:], in1=ps[:, 0:512], op=ALU.add), reads=[ps, x_], writes=[x_])
            kb.dma(xnext[blk * 128:(blk + 1) * 128, half * 512:(half + 1) * 512], x_[:], reads=[x_])
    print("LF", kb.S.emit())
    return kb.nc


def host_LF_inputs(l, xmid, inp):
    consts = const_arrays()
    g8 = np.ascontiguousarray(inp["ffn_norm_g"][l].reshape(8, 128).T)
    cwv = np.concatenate([inp["conv_w"][l], inp["conv_b"][l][None, :]], axis=0)
    cw = np.ascontiguousarray(cwv.reshape(4, 44, 128).transpose(2, 1, 0).reshape(128, 44 * 4))
    maps = []
    for c in range(NCORE):
        b, j = c // 4, c % 4
        xe = np.zeros((NE, D), np.float32)
        xe[128:] = xmid[b, j * NT:(j + 1) * NT]
        if j > 0:
            xe[:128] = xmid[b, j * NT - 128:j * NT]
        m = dict(consts)
        m.update(xme=xe, wup=np.ascontiguousarray(inp["w_up"][l]), wdn=np.ascontiguousarray(inp["w_down"][l]), g8=g8, cw=cw)
        maps.append(m)
    return maps


def gather_rows(res, key):
    out = np.zeros((2, 4 * NT, res[0][key].shape[1]), np.float32)
    for c in range(NCORE):
        b, j = c // 4, c % 4
        out[b, j * NT:(j + 1) * NT] = res[c][key]
    return out


def kernel(**inputs):
    inp = {k: np.asarray(v) for k, v in inputs.items()}
    x = np.ascontiguousarray(inp["x"], dtype=np.float32)
    vfirstT = None
    for l in range(2):
        la = run(get_prog("LA", build_LA), host_LA_inputs(l, x, inp))
        rs = run(get_prog("LRS%d" % l, lambda: build_LRS(l > 0)), host_LRS_inputs(l, x, inp, vfirstT))
        if l == 0:
            vfirstT = [rs[c]["vT"] for c in range(NCORE)]
        lb = run(get_prog("LB", build_LB), host_LB_inputs(l, x, inp, rs, la))
        xmid = gather_rows(lb, "xmid")
        lf = run(get_prog("LF", build_LF), host_LF_inputs(l, xmid, inp))
        x = gather_rows(lf, "xnext")
    return x.astype(np.float32)
```
